# Optimizing a Trainium2 kernel written in Bass

```python
import jax, jax.numpy as jnp
from jax import lax
import numpy as np

D_MODEL = 1024
BATCH = 8
SEQ = 4096
DEPTH = 2

CHUNK = 64
GLA_HEADS = 4
KEY_WIDTH = D_MODEL // 2
VAL_WIDTH = D_MODEL
HEAD_K = KEY_WIDTH // GLA_HEADS
HEAD_V = VAL_WIDTH // GLA_HEADS
GATE_RANK = 16
GATE_TAU = 16.0
CONV_CH = D_MODEL
CONV_WIDTH = 3
FFN_HIDDEN = -(-8 * D_MODEL // (3 * 256)) * 256
IN_WIDTH = 2 * KEY_WIDTH + 2 * VAL_WIDTH + GATE_RANK + 3 * CONV_CH + 2 * D_MODEL
NORM_EPS = 1e-6

kernel_name = "gla_shortconv_gated_hybrid"


def _split_points():
    sizes = (KEY_WIDTH, KEY_WIDTH, VAL_WIDTH, VAL_WIDTH, GATE_RANK,
             CONV_CH, CONV_CH, CONV_CH, D_MODEL, D_MODEL)
    return tuple(int(v) for v in np.cumsum(sizes)[:-1])


def rmsnorm(x, g):
    xf = x.astype(jnp.float32)
    y = xf * lax.rsqrt(jnp.mean(xf * xf, axis=-1, keepdims=True) + NORM_EPS)
    return (y * g.astype(jnp.float32)).astype(x.dtype)


def gla_chunk_causal(q, k, v, log_a):
    b_sz, s_len = q.shape[0], q.shape[1]
    n_chunks = s_len // CHUNK

    def to_chunks(t):
        return t.astype(jnp.float32).reshape(b_sz, n_chunks, CHUNK, GLA_HEADS, t.shape[-1]).transpose(1, 0, 3, 2, 4)

    qc, kc, vc, lc = to_chunks(q), to_chunks(k), to_chunks(v), to_chunks(log_a)
    cum = jnp.cumsum(lc, axis=3)
    cum_end = cum[:, :, :, -1:, :]
    kd = kc * jnp.exp(cum_end - cum)
    gamma = jnp.exp(cum_end[:, :, :, 0, :])

    def step(state, inp):
        q_i, kd_i, v_i, g_i = inp
        state = g_i[..., None] * state + jnp.einsum('bhlk,bhlv->bhkv', kd_i, v_i)
        o_i = jnp.einsum('bhlk,bhkv->bhlv', q_i, state)
        return state, o_i

    s0 = jnp.zeros((b_sz, GLA_HEADS, HEAD_K, HEAD_V), jnp.float32)
    _, o = lax.scan(step, s0, (qc, kd, vc, gamma))
    return o.transpose(1, 0, 3, 2, 4).reshape(b_sz, s_len, GLA_HEADS, HEAD_V)


def causal_depthwise_conv(u, w, bias):
    rhs = w.astype(u.dtype)[:, None, :]
    y = lax.conv_general_dilated(u, rhs, window_strides=(1,), padding=[(CONV_WIDTH - 1, 0)],
                                 dimension_numbers=('NWC', 'WIO', 'NWC'),
                                 feature_group_count=u.shape[-1])
    return y + bias.astype(u.dtype)


def setup_inputs(seed: int = 0) -> dict:
    key = jax.random.key(seed)
    ks = jax.random.split(key, 20)

    def nrm(k, shape, scale):
        return jax.random.normal(k, shape, jnp.float32) * scale

    def gain(k, shape):
        return 1.0 + 0.02 * jax.random.normal(k, shape, jnp.float32)

    return {
        "x": nrm(ks[0], (BATCH, SEQ, D_MODEL), 1.0),
        "norm1_g": gain(ks[1], (DEPTH, D_MODEL)),
        "w_in": nrm(ks[2], (DEPTH, D_MODEL, IN_WIDTH), D_MODEL ** -0.5),
        "w_fg2": nrm(ks[3], (DEPTH, GATE_RANK, KEY_WIDTH), GATE_RANK ** -0.5),
        "b_fg": nrm(ks[4], (DEPTH, KEY_WIDTH), 0.01),
        "gla_norm_g": gain(ks[5], (DEPTH, HEAD_V)),
        "w_oa": nrm(ks[6], (DEPTH, VAL_WIDTH, D_MODEL), VAL_WIDTH ** -0.5),
        "conv_w": nrm(ks[7], (DEPTH, CONV_WIDTH, CONV_CH), CONV_WIDTH ** -0.5),
        "conv_b": nrm(ks[8], (DEPTH, CONV_CH), 0.01),
        "w_ob": nrm(ks[9], (DEPTH, CONV_CH, D_MODEL), CONV_CH ** -0.5),
        "w_o": nrm(ks[10], (DEPTH, D_MODEL, D_MODEL), D_MODEL ** -0.5),
        "norm2_g": gain(ks[11], (DEPTH, D_MODEL)),
        "w_ffn_gate": nrm(ks[12], (DEPTH, D_MODEL, FFN_HIDDEN), D_MODEL ** -0.5),
        "w_ffn_up": nrm(ks[13], (DEPTH, D_MODEL, FFN_HIDDEN), D_MODEL ** -0.5),
        "w_ffn_down": nrm(ks[14], (DEPTH, FFN_HIDDEN, D_MODEL), FFN_HIDDEN ** -0.5),
        "final_g": gain(ks[15], (D_MODEL,)),
    }


def reference(x, norm1_g, w_in, w_fg2, b_fg, gla_norm_g, w_oa, conv_w, conv_b, w_ob, w_o,
              norm2_g, w_ffn_gate, w_ffn_up, w_ffn_down, final_g):
    b_sz, s_len, _ = x.shape
    split_pts = _split_points()
    for l in range(DEPTH):
        h = rmsnorm(x, norm1_g[l])
        proj = jnp.einsum('bsd,de->bse', h, w_in[l])
        q, k, v, r, fz, gb_in, gc_in, cx, ga, gb = jnp.split(proj, split_pts, axis=-1)

        fg = jnp.einsum('bsr,rk->bsk', fz, w_fg2[l]) + b_fg[l]
        log_a = jax.nn.log_sigmoid(fg.astype(jnp.float32)) / GATE_TAU
        qh = q.reshape(b_sz, s_len, GLA_HEADS, HEAD_K) * (HEAD_K ** -0.5)
        kh = k.reshape(b_sz, s_len, GLA_HEADS, HEAD_K)
        vh = v.reshape(b_sz, s_len, GLA_HEADS, HEAD_V)
        ah = log_a.reshape(b_sz, s_len, GLA_HEADS, HEAD_K)
        o = gla_chunk_causal(qh, kh, vh, ah)
        o = o * lax.rsqrt(jnp.mean(o * o, axis=-1, keepdims=True) + NORM_EPS) * gla_norm_g[l].astype(jnp.float32)
        o = o.reshape(b_sz, s_len, VAL_WIDTH).astype(x.dtype) * jax.nn.silu(r)
        y_a = jnp.einsum('bsv,vd->bsd', o, w_oa[l])

        conv = causal_depthwise_conv(gc_in * cx, conv_w[l], conv_b[l])
        y_b = jnp.einsum('bsc,cd->bsd', gb_in * conv, w_ob[l])

        mix = jax.nn.sigmoid(ga) * y_a + jax.nn.sigmoid(gb) * y_b
        x = x + jnp.einsum('bsd,de->bse', mix, w_o[l])

        h2 = rmsnorm(x, norm2_g[l])
        hid = jax.nn.silu(jnp.einsum('bsd,df->bsf', h2, w_ffn_gate[l])) * jnp.einsum('bsd,df->bsf', h2, w_ffn_up[l])
        x = x + jnp.einsum('bsf,fd->bsd', hid, w_ffn_down[l])
    return rmsnorm(x, final_g)
```

```python
import numpy as np
import concourse.bass as bass
import concourse.mybir as mybir
from concourse.bass_utils import run_bass_kernel_spmd
from contextlib import ExitStack

F32 = mybir.dt.float32
BF16 = mybir.dt.bfloat16
AF = mybir.ActivationFunctionType
ALU = mybir.AluOpType

D = 1024
KC = 8
T = 512
NTS = 4
DEPTH = 2
INW = 8208
FF = 2816
NFC = 22
EPS = 1e-6
NW = 4
NCORES = 8


class _Eng:
    def __init__(self, name, h, sem):
        self.name, self.h, self.sem = name, h, sem
        self.cnt = 0
        self.waited = {}


class Trk:
    def __init__(self, nc, es):
        self.nc = nc
        self.es = es
        self.reg = {}
        self.E = {}
        for n, h in (("pe", nc.tensor), ("act", nc.scalar), ("dve", nc.vector),
                     ("pool", nc.gpsimd), ("sp", nc.sync)):
            self.E[n] = _Eng(n, h, es.enter_context(nc.semaphore("s_" + n)))
        self.nsem = 0

    def newsem(self, name):
        self.nsem += 1
        return self.es.enter_context(self.nc.semaphore(name))

    def deps(self, e, R, W):
        need = {}

        def add(ev, raw):
            if ev is None:
                return
            sem, val, src = ev
            if src == e.name:
                if e.name in ("pe", "sp"):
                    return
            k = id(sem)
            if k not in need or need[k][1] < val:
                need[k] = (sem, val)

        for k in R:
            st = self.reg.get(k)
            if st is not None:
                add(st[0], True)
        for k in W:
            st = self.reg.get(k)
            if st is not None:
                add(st[0], False)
                for ev in st[1].values():
                    add(ev, False)
        for k, (sem, val) in need.items():
            if e.waited.get(k, 0) < val:
                e.h.wait_ge(sem, val)
                e.waited[k] = val

    def record(self, R, W, ev):
        for k in R:
            st = self.reg.get(k)
            if st is None:
                st = [None, {}]
                self.reg[k] = st
            key = id(ev[0])
            old = st[1].get(key)
            if old is None or old[1] < ev[1]:
                st[1][key] = ev
        for k in W:
            self.reg[k] = [ev, {}]

    def op(self, en, R, W, f):
        e = self.E[en]
        self.deps(e, R, W)
        ins = f(e.h)
        ins.then_inc(e.sem, 1)
        e.cnt += 1
        self.record(R, W, (e.sem, e.cnt, en))

    def mm_group(self, bank_key, mms):
        e = self.E["pe"]
        n = len(mms)
        bks = bank_key if isinstance(bank_key, list) else [bank_key]
        for i, (f, R) in enumerate(mms):
            self.deps(e, R, bks if i == 0 else [])
            ins = f(e.h)
            if i == n - 1:
                ins.then_inc(e.sem, 1)
                e.cnt += 1
                ev = (e.sem, e.cnt, "pe")
            else:
                ev = (e.sem, e.cnt + 1, "pe")
            self.record(R, [], ev)
        self.record([], bks, (e.sem, e.cnt, "pe"))

    def dma(self, qn, R, W, dsem, f):
        e = self.E[qn]
        self.deps(e, R, W)
        ins = f(e.h)
        ins.then_inc(dsem[0], 16)
        dsem[1] += 16
        self.record(R, W, (dsem[0], dsem[1], "dma"))


def _block_list():
    bl = []
    inw = lambda c0, n: [("w_in", c0, n, 0)]
    bl.append(("fz", 8, 16, inw(3072, 16)))
    bl.append(("q", 8, 512, inw(0, 512)))
    bl.append(("v0", 8, 512, inw(1024, 512)))
    bl.append(("v1", 8, 512, inw(1536, 512)))
    bl.append(("r0", 8, 512, inw(2048, 512)))
    bl.append(("r1", 8, 512, inw(2560, 512)))
    bl.append(("k", 8, 512, inw(512, 512)))
    for hf in range(2):
        bl.append(("C%d" % hf, 8, 512, inw(4112 + hf * 512, 512)))
        bl.append(("X%d" % hf, 8, 512, inw(5136 + hf * 512, 512)))
        bl.append(("B%d" % hf, 8, 512, inw(3088 + hf * 512, 512)))
    for hf in range(2):
        bl.append(("ga%d" % hf, 8, 512, inw(6160 + hf * 512, 512)))
    for hf in range(2):
        bl.append(("gb%d" % hf, 8, 512, inw(7184 + hf * 512, 512)))
    for hf in range(2):
        bl.append(("oa%d" % hf, 8, 512, [("w_oa", hf * 512, 512, 0)]))
        bl.append(("ob%d" % hf, 8, 512, [("w_ob", hf * 512, 512, 0)]))
    for hf in range(2):
        bl.append(("o%d" % hf, 8, 512, [("w_o", hf * 512, 512, 0)]))
    for p in range(11):
        bl.append(("f%d" % p, 8, 512, [("w_ffn_gate", p * 256, 256, 0), ("w_ffn_up", p * 256, 256, 256)]))
    for dc in range(8):
        bl.append(("d%d" % dc, NFC, 128, [("w_ffn_down", dc * 128, 128, 0)]))
    return bl


def build_nc(S, depth=DEPTH):
    NT = S // T
    nc = bass.Bass("TRN2", target_bir_lowering=False)
    dt_in = lambda n, s: nc.dram_tensor(n, s, F32, kind="ExternalInput").ap()
    x_d = dt_in("x", [S, D])
    wd = {
        "w_in": dt_in("w_in", [DEPTH, D, INW]),
        "w_oa": dt_in("w_oa", [DEPTH, D, D]),
        "w_ob": dt_in("w_ob", [DEPTH, D, D]),
        "w_o": dt_in("w_o", [DEPTH, D, D]),
        "w_ffn_gate": dt_in("w_ffn_gate", [DEPTH, D, FF]),
        "w_ffn_up": dt_in("w_ffn_up", [DEPTH, D, FF]),
        "w_ffn_down": dt_in("w_ffn_down", [DEPTH, FF, D]),
    }
    wfg_d = dt_in("wfg", [DEPTH, 17, 512])
    pk_d = dt_in("pk", [128, 108])
    cst_d = dt_in("cst", [128, 258])
    y_d = nc.dram_tensor("y", [S, D], F32, kind="ExternalOutput").ap()

    blocks = _block_list()
    NB = len(blocks)
    scr = {}
    for l in range(depth):
        for bi, (bn, kcs, ncol, srcs) in enumerate(blocks):
            scr[(l, bi)] = nc.dram_tensor("sc_%d_%s" % (l, bn), [128, kcs * ncol], BF16).ap()

    with ExitStack() as es:
        tk = Trk(nc, es)
        op, mmg, dma = tk.op, tk.mm_group, tk.dma
        sb = lambda n, s, d: es.enter_context(nc.sbuf_tensor("sb_" + n, s, d))

        xT = sb("xT", [128, KC, T], F32)
        hT = sb("hT", [128, KC, T], BF16)
        wsl = [sb("wsl%d" % i, [128, 4096], BF16) for i in range(NW)]
        sqb = [sb("sqb%d" % i, [128, T], BF16) for i in range(2)]
        rstd = sb("rstd", [128, T], F32)
        qT = sb("qT", [128, 4, T], BF16)
        fza = sb("fza", [32, T], F32)
        eb = [sb("eb%d" % i, [128, T], F32) for i in range(2)]
        spb = [sb("spb%d" % i, [128, T], F32) for i in range(4)]
        gam = sb("gam", [128, NTS, 8], F32)
        kd = sb("kd", [128, NTS, 512], BF16)
        S32 = [[sb("S32_%d_%d" % (l, h), [128, 256], F32) for h in range(4)] for l in range(depth)]
        S16 = [sb("S16_%d" % i, [128, 256], BF16) for i in range(8)]
        ogtok = [sb("ogtok%d" % i, [128, 1024], BF16) for i in range(2)]
        ss = [sb("ss%d" % i, [128, 4], F32) for i in range(2)]
        rs2 = [sb("rs2%d" % i, [128, 4], F32) for i in range(2)]
        sqjunk = [sb("sqjunk%d" % i, [128, 256], BF16) for i in range(4)]
        ln2 = [sb("ln2%d" % i, [128, 4], F32) for i in range(2)]
        ogT = sb("ogT", [128, KC, T], BF16)
        csb = [sb("csb%d" % i, [128, T], F32) for i in range(2)]
        ub = [sb("ub%d" % i, [128, T + 2], F32) for i in range(2)]
        cacc = [sb("cacc%d" % i, [128, T], F32) for i in range(2)]
        halo = [sb("halo%d" % l, [128, KC, 2], F32) for l in range(depth)]
        gbcT = sb("gbcT", [128, KC, T], BF16)
        tga = sb("tga", [128, KC, T], BF16)
        tgb = sb("tgb", [128, KC, T], BF16)
        mixT = sb("mixT", [128, KC, T], BF16)
        ident32 = sb("ident32", [128, 128], F32)
        ident16 = sb("ident16", [128, 128], BF16)
        ones16 = sb("ones16", [128, 128], BF16)
        msk = sb("msk", [128, 130], F32)
        pk = sb("pk", [128, 108], F32)
        wfg = sb("wfg", [32, DEPTH, 512], F32)
        xst = [sb("xst%d" % i, [128, 4, 256], F32) for i in range(2)]
        yst = [sb("yst%d" % i, [128, 4, 128], F32) for i in range(3)]
        arena = sb("arena", [128, 6144], F32)

        def AR(i, n=1):
            return [("ar", j) for j in range(i, i + n)]

        a16 = arena[:, :].bitcast(BF16)
        v_v = a16[:, 0:4096].rearrange("p (a b) -> p a b", a=4)
        grs_v = a16[:, 4096:8192].rearrange("p (a b) -> p a b", a=4)
        dec_v = arena[:, 4096:6144].rearrange("p (a b) -> p a b", a=4)
        hid_v = a16[:, 0:NFC * 512].rearrange("p (a b) -> p a b", a=NFC)

        pb = [es.enter_context(nc.psum_tensor("pb%d" % i, [128, 512], F32)) for i in range(8)]
        bank_i = [0]
        ring = [list(range(8))]

        def set_ring(lst):
            ring[0] = lst
            bank_i[0] = 0

        def bank():
            i = bank_i[0] % len(ring[0])
            bank_i[0] = i + 1
            return ring[0][i]

        def interleave(ga, gb, nbp):
            a_done = b_done = False
            it = 0
            while not (a_done and b_done):
                if not a_done:
                    try:
                        next(ga)
                    except StopIteration:
                        a_done = True
                nb = nbp[it % len(nbp)] if isinstance(nbp, list) else nbp
                it += 1
                for _ in range(nb):
                    if not b_done:
                        try:
                            next(gb)
                        except StopIteration:
                            b_done = True

        BK = lambda i: ("bank", i)

        cs = [tk.newsem("cs"), 0]
        dma("sp", [], ["ident32"], cs, lambda q: q.dma_start(out=ident32[:], in_=cst_d[:, 0:128]))
        cs2 = [tk.newsem("cs2"), 0]
        dma("sp", [], ["msk"], cs2, lambda q: q.dma_start(out=msk[:], in_=cst_d[:, 128:258]))
        cs3 = [tk.newsem("cs3"), 0]
        dma("sp", [], ["pk"], cs3, lambda q: q.dma_start(out=pk[:], in_=pk_d))
        cs4 = [tk.newsem("cs4"), 0]
        dma("sp", [], ["wfg"], cs4, lambda q: q.dma_start(out=wfg[0:17, :, :], in_=wfg_d.rearrange("l r k -> r l k")))
        op("dve", ["ident32"], ["ident16"], lambda h: h.tensor_copy(out=ident16[:], in_=ident32[:]))
        op("dve", [], ["ones16"], lambda h: h.memset(ones16[:], 1.0 / D))
        op("dve", [], ["fza"], lambda h: h.memset(fza[:], 1.0))
        for l in range(depth):
            for hh in range(4):
                op("dve", [], [("S32", l, hh)], lambda h, l=l, hh=hh: h.memset(S32[l][hh][:], 0.0))
            op("dve", [], [("halo", l)], lambda h, l=l: h.memset(halo[l][:], 0.0))

        NPS = 16
        psem = [[tk.newsem("ps%d" % i), 0] for i in range(NPS)]
        pi = 0
        for l in range(depth):
            for bi, (bn, kcs, ncol, srcs) in enumerate(blocks):
                dst = scr[(l, bi)].rearrange("p (kc j) -> p kc j", kc=kcs)
                for si, (sn, c0, n, d0) in enumerate(srcs):
                    src = wd[sn][l][:, c0:c0 + n].rearrange("(kc p) j -> p kc j", p=128)
                    for ci, k0 in enumerate(range(0, kcs, 4)):
                        k1 = min(kcs, k0 + 4)
                        ds = psem[pi % (2 if pi < 8 else NPS)]
                        pi += 1
                        e = tk.E["pool"]
                        if ds[1] > 0 and e.waited.get(id(ds[0]), 0) < ds[1]:
                            e.h.wait_ge(ds[0], ds[1])
                            e.waited[id(ds[0])] = ds[1]
                        dma("pool", [], [("scr", l, bi, si, ci)], ds,
                            lambda q, dst=dst, src=src, d0=d0, n=n, k0=k0, k1=k1: q.dma_start(out=dst[:, k0:k1, d0:d0 + n], in_=src[:, k0:k1, :]))

        seq = [(t, l, bi) for t in range(NT) for l in range(depth) for bi in range(NB)]
        wst = {"next_load": 0, "free": list(range(NW)), "loaded": {}}
        wsem = [[tk.newsem("ws%d" % i), 0] for i in range(NW)]

        def issue_loads():
            while wst["free"] and wst["next_load"] < len(seq):
                j = wst["next_load"]
                wst["next_load"] += 1
                t, l, bi = seq[j]
                s = wst["free"].pop(0)
                bn, kcs, ncol, srcs = blocks[bi]
                R = [("scr", l, bi, si, ci) for si in range(len(srcs)) for ci in range((kcs + 3) // 4)]
                dma("sp", R, [("ws", s)], wsem[s],
                    lambda q, s=s, l=l, bi=bi, kcs=kcs, ncol=ncol: q.dma_start(out=wsl[s][:, 0:kcs * ncol], in_=scr[(l, bi)]))
                wst["loaded"][j] = s

        cur = [0]

        def get_block(name):
            j = cur[0]
            cur[0] += 1
            t, l, bi = seq[j]
            assert blocks[bi][0] == name, (blocks[bi][0], name)
            if j not in wst["loaded"]:
                issue_loads()
            s = wst["loaded"].pop(j)
            kcs, ncol = blocks[bi][1], blocks[bi][2]
            return s, wsl[s][:, 0:kcs * ncol].rearrange("p (kc j) -> p kc j", kc=kcs)

        def done_block(s):
            wst["free"].append(s)
            issue_loads()

        issue_loads()

        def pkc(c):
            return pk[:, c:c + 1]

        def mm(out, lhsT, rhs, first, last):
            return lambda pe: pe.matmul(out, lhsT, rhs, start=first, stop=last)

        def fm_group(b, w, ws_key, col0, rhsT, rkeys, m=128):
            mmg(BK(b), [(mm(pb[b][0:m, :], w[:, kc, col0:col0 + m], rhsT[:, kc, :], kc == 0, kc == KC - 1),
                         [ws_key] + rkeys(kc)) for kc in range(KC)])

        def fm_groups_kco(specs, rhsT, rkeys):
            e = tk.E["pe"]
            for kc in range(KC):
                for (b, w, ws_key, col0, m) in specs:
                    R = [ws_key] + rkeys(kc)
                    tk.deps(e, R, [BK(b)] if kc == 0 else [])
                    ins = e.h.matmul(pb[b][0:m, :], w[:, kc, col0:col0 + m], rhsT[:, kc, :], start=(kc == 0), stop=(kc == KC - 1))
                    if kc == KC - 1:
                        ins.then_inc(e.sem, 1)
                        e.cnt += 1
                        tk.record(R, [BK(b)], (e.sem, e.cnt, "pe"))
                    else:
                        tk.record(R, [], (e.sem, e.cnt + 1, "pe"))

        def tm_group(b, w, ws_key, ts, lhsT, lkeys):
            mmg(BK(b), [(mm(pb[b][:, :], lhsT[:, kc, ts * 128:(ts + 1) * 128], w[:, kc, :], kc == 0, kc == KC - 1),
                         [ws_key] + lkeys(kc)) for kc in range(KC)])

        hk = lambda kc: [("hT", kc)]
        sq_i = [0]

        def rmsnorm(gcol, out_fn):
            bn = bank()
            e = tk.E["pe"]
            for dc in range(KC):
                sqi = sq_i[0] % 2
                sq_i[0] += 1
                op("act", [("xT", dc)], [("sqb", sqi)],
                   lambda h, dc=dc, sqi=sqi: h.activation(out=sqb[sqi][:], in_=xT[:, dc, :], func=AF.Square))
                R = ["ones16", ("sqb", sqi)]
                tk.deps(e, R, [BK(bn)] if dc == 0 else [])
                ins = e.h.matmul(pb[bn][:, :], ones16[:, :], sqb[sqi][:], start=(dc == 0), stop=(dc == KC - 1))
                ins.then_inc(e.sem, 1)
                e.cnt += 1
                tk.record(R, [BK(bn)] if dc == KC - 1 else [], (e.sem, e.cnt, "pe"))
            op("act", [BK(bn)], ["rstd", BK(bn)],
               lambda h: h.activation(out=rstd[:], in_=pb[bn][:, :], func=AF.Ln, bias=EPSB[:, 0:1]))
            op("act", ["rstd"], ["rstd"], lambda h: h.activation(out=rstd[:], in_=rstd[:], func=AF.Exp, scale=-0.5))
            for dc in range(KC):
                out_fn(dc)

        EPSB = sb("epsb", [128, 1], F32)
        op("dve", [], ["epsb"], lambda h: h.memset(EPSB[:], EPS))

        alt = [0]

        def evac_engine():
            alt[0] += 1
            return "act" if alt[0] % 2 else "dve"

        ysem = [[tk.newsem("ys%d" % i), 0] for i in range(3)]
        xsem = {q: [[tk.newsem("xs%s%d" % (q, i)), 0] for i in range(2)] for q in ("act", "pool")}

        def x_load(t, pc, q):
            t0 = t * T
            dma(q, [], [("xst", pc % 2)], xsem[q][pc % 2],
                lambda qq, t0=t0, pc=pc: qq.dma_start(out=xst[pc % 2][:], in_=x_d[t0:t0 + T, pc * 256:(pc + 1) * 256].rearrange("(a p) d -> p a d", p=128)))

        def x_transposes(pc):
            for j in range(2):
                dc = 2 * pc + j
                b = bank()
                mmg(BK(b), [(lambda pe, b=b, ts=ts, j=j, pc=pc: pe.transpose(pb[b][:, ts * 128:(ts + 1) * 128],
                                                                           xst[pc % 2][:, ts, j * 128:(j + 1) * 128], ident32[:]),
                             ["ident32", ("xst", pc % 2)]) for ts in range(NTS)])
                en = evac_engine()
                if en == "act":
                    op("act", [BK(b)], [("xT", dc), BK(b)], lambda h, b=b, dc=dc: h.activation(out=xT[:, dc, :], in_=pb[b][:, :], func=AF.Copy))
                else:
                    op("dve", [BK(b)], [("xT", dc), BK(b)], lambda h, b=b, dc=dc: h.tensor_copy(out=xT[:, dc, :], in_=pb[b][:, :]))

        for t in range(NT):
            t0 = t * T
            if t == 0:
                x_load(0, 0, "act")
                x_load(0, 1, "act")
                x_transposes(0)
                x_load(0, 2, "act")
                x_transposes(1)
                x_load(0, 3, "act")
                x_transposes(2)
                x_transposes(3)
                if NT > 1:
                    x_load(1, 0, "pool")
                    x_load(1, 1, "pool")

            for l in range(depth):
                def h_out(dc, l=l):
                    op("dve", [("xT", dc), "rstd", "pk"], [("hT", dc)],
                       lambda h, dc=dc: h.scalar_tensor_tensor(out=hT[:, dc, :], in0=xT[:, dc, :], scalar=pkc(l * 8 + dc),
                                                               in1=rstd[:], op0=ALU.mult, op1=ALU.mult))
                rmsnorm(None, h_out)

                s, w = get_block("fz")
                sq_, wq_ = get_block("q")
                b = bank()
                qb = [bank() for _ in range(4)]
                fm_groups_kco([(b, w, ("ws", s), 0, 16)] + [(qb[hh], wq_, ("ws", sq_), hh * 128, 128) for hh in range(4)], hT, hk)
                done_block(s)
                done_block(sq_)
                op("dve", [BK(b)], ["fza", BK(b)], lambda h, b=b: h.tensor_copy(out=fza[0:16, :], in_=pb[b][0:16, :]))
                for hh in range(4):
                    op("act", [BK(qb[hh])], [("qT", hh), BK(qb[hh])],
                       lambda h, hh=hh, qb=qb: h.activation(out=qT[:, hh, :], in_=pb[qb[hh]][:, :], func=AF.Copy, scale=128.0 ** -0.5))
                for ts in range(NTS):
                    b = bank()
                    mmg(BK(b), [(mm(pb[b][:, :], fza[0:17, ts * 128:(ts + 1) * 128], wfg[0:17, l, :], True, True), ["fza", "wfg"])])
                    ei = ts % 2
                    op("act", [BK(b)], [("eb", ei), BK(b)], lambda h, b=b, ei=ei: h.activation(out=eb[ei][:], in_=pb[b][:, :], func=AF.Exp, scale=-1.0))
                    op("act", [("eb", ei)], [("spb", ts)], lambda h, ts=ts, ei=ei: h.activation(out=spb[ts][:], in_=eb[ei][:], func=AF.Ln, bias=1.0))

                def gen_gate(l=l):
                    for ts in range(NTS):
                        b = bank()
                        mmg(BK(b), [(mm(pb[b][:, :], msk[:, 0:128], spb[ts][:], True, True), ["msk", ("spb", ts)])])
                        op("act", [BK(b)], AR(16 + 2 * ts, 2) + [BK(b)],
                           lambda h, b=b, ts=ts: h.activation(out=dec_v[:, ts, :], in_=pb[b][:, :], func=AF.Exp))
                        b = bank()
                        mmg(BK(b), [(mm(pb[b][:, hh * 2:hh * 2 + 2], spb[ts][:, hh * 128:(hh + 1) * 128], msk[:, 128:130], True, True),
                                     ["msk", ("spb", ts)]) for hh in range(4)])
                        op("act", [BK(b)], [("gam", ts), BK(b)],
                           lambda h, b=b, ts=ts: h.activation(out=gam[:, ts, :], in_=pb[b][:, 0:8], func=AF.Exp))
                        yield

                def gen_inproj(l=l):
                    for hf in range(2):
                        s, w = get_block("v%d" % hf)
                        for ts in range(NTS):
                            b = bank()
                            tm_group(b, w, ("ws", s), ts, hT, hk)
                            en = evac_engine()
                            if en == "act":
                                op("act", [BK(b)], AR(2 * ts + hf, 1) + [BK(b)],
                                   lambda h, b=b, ts=ts, hf=hf: h.activation(out=v_v[:, ts, hf * 512:(hf + 1) * 512], in_=pb[b][:, :], func=AF.Copy))
                            else:
                                op("dve", [BK(b)], AR(2 * ts + hf, 1) + [BK(b)],
                                   lambda h, b=b, ts=ts, hf=hf: h.tensor_copy(out=v_v[:, ts, hf * 512:(hf + 1) * 512], in_=pb[b][:, :]))
                            yield
                        done_block(s)
                    for hf in range(2):
                        s, w = get_block("r%d" % hf)
                        for ts in range(NTS):
                            b = bank()
                            tm_group(b, w, ("ws", s), ts, hT, hk)
                            op("act", [BK(b)], AR(8 + 2 * ts + hf, 1) + [BK(b)],
                               lambda h, b=b, ts=ts, hf=hf: h.activation(out=grs_v[:, ts, hf * 512:(hf + 1) * 512], in_=pb[b][:, :], func=AF.Silu))
                            yield
                        done_block(s)

                interleave(gen_gate(), gen_inproj(), 2)
                s, w = get_block("k")
                for ts in range(NTS):
                    b = bank()
                    tm_group(b, w, ("ws", s), ts, hT, hk)
                    op("dve", [BK(b)] + AR(16 + 2 * ts, 2), [("kd", ts), BK(b)],
                       lambda h, b=b, ts=ts: h.tensor_tensor(out=kd[:, ts, :], in0=pb[b][:, :], in1=dec_v[:, ts, :], op=ALU.mult))
                done_block(s)

                def gen_gla(l=l):
                    s16c = [0]

                    def emit_T(ts):
                        ri = ts % 2
                        tb = 3
                        tb16 = pb[tb][:, :].bitcast(BF16)
                        mmg(BK(tb), [(lambda pe, vc=vc, ri=ri, tb16=tb16: pe.transpose(tb16[:, vc * 128:(vc + 1) * 128],
                                                                                     ogtok[ri][:, vc * 128:(vc + 1) * 128], ident16[:]),
                                      ["ident16", ("ogtok", ri, vc // 2)]) for vc in range(8)])
                        for par in range(2):
                            src = tb16.rearrange("p (a b c) -> p a b c", a=4, b=2)[:, :, par, :]
                            dst = ogT[:, :, ts * 128:(ts + 1) * 128].rearrange("p (a b) c -> p a b c", b=2)[:, :, par, :]
                            op("dve", [BK(tb), "pk"], [("ogT", ts, par), BK(tb)],
                               lambda h, src=src, dst=dst, par=par: h.tensor_scalar_mul(out=dst, in0=src, scalar1=pkc(96 + l * 2 + par)))

                    slots_of = {}

                    def AB(ts, c):
                        pr = slice(c * 64, (c + 1) * 64)
                        for hp in range(2):
                            ubk = 2 + hp
                            mmg(BK(ubk), [(mm(pb[ubk][:, h2 * 256:(h2 + 1) * 256], kd[pr, ts, (hp * 2 + h2) * 128:(hp * 2 + h2 + 1) * 128],
                                              v_v[pr, ts, (hp * 2 + h2) * 256:(hp * 2 + h2 + 1) * 256], True, True),
                                           [("kd", ts)] + AR(2 * ts + hp, 1)) for h2 in range(2)])
                        slots = []
                        for hp in range(2):
                            ubk = 2 + hp
                            for h2 in range(2):
                                hh = hp * 2 + h2
                                gcol = hh * 2 + c
                                op("dve", [BK(ubk), ("S32", l, hh), ("gam", ts)], [("S32", l, hh), BK(ubk)],
                                   lambda h, ubk=ubk, h2=h2, hh=hh, gcol=gcol, ts=ts: h.scalar_tensor_tensor(
                                       out=S32[l][hh][:], in0=S32[l][hh][:], scalar=gam[:, ts, gcol:gcol + 1],
                                       in1=pb[ubk][:, h2 * 256:(h2 + 1) * 256], op0=ALU.mult, op1=ALU.add))
                                si = s16c[0] % 8
                                s16c[0] += 1
                                op("act", [("S32", l, hh)], [("S16", si)],
                                   lambda h, si=si, hh=hh: h.activation(out=S16[si][:], in_=S32[l][hh][:], func=AF.Copy))
                                slots.append((hh, si))
                        slots_of[(ts, c)] = slots

                    def C(ts, c):
                        pr = slice(c * 64, (c + 1) * 64)
                        slots = slots_of.pop((ts, c))
                        for hp in range(2):
                            mmg(BK(hp), [(mm(pb[hp][pr, (hh % 2) * 256:(hh % 2 + 1) * 256],
                                             qT[:, hh, ts * 128 + c * 64: ts * 128 + (c + 1) * 64], S16[si][:], True, True),
                                          [("qT", hh), ("S16", si)]) for hh, si in slots[2 * hp:2 * hp + 2]])

                    def E(ts):
                        ri = ts % 2
                        for hp in range(2):
                            for hh in (2 * hp, 2 * hp + 1):
                                op("act", [BK(hp)], [("sqjunk", hh), ("ss", ri, hh), BK(hp)],
                                   lambda h, hh=hh, ri=ri, hp=hp: h.activation(out=sqjunk[hh][:], in_=pb[hp][:, (hh % 2) * 256:(hh % 2 + 1) * 256],
                                                                        func=AF.Square, accum_out=ss[ri][:, hh:hh + 1]))
                            op("act", [("ss", ri, 2 * hp), ("ss", ri, 2 * hp + 1)], [("ss2", ri, hp)],
                               lambda h, ri=ri, hp=hp: h.activation(out=ln2[ri][:, 2 * hp:2 * hp + 2], in_=ss[ri][:, 2 * hp:2 * hp + 2], func=AF.Ln,
                                                                    scale=1.0 / 256, bias=EPSB[:, 0:1]))
                            op("act", [("ss2", ri, hp)], [("rs2", ri, hp)],
                               lambda h, ri=ri, hp=hp: h.activation(out=rs2[ri][:, 2 * hp:2 * hp + 2], in_=ln2[ri][:, 2 * hp:2 * hp + 2], func=AF.Exp, scale=-0.5))
                            for hh in (2 * hp, 2 * hp + 1):
                                op("dve", [BK(hp), ("rs2", ri, hp)] + AR(8 + 2 * ts + hp, 1), [("ogtok", ri, hh), BK(hp)],
                                   lambda h, hh=hh, ri=ri, ts=ts, hp=hp: h.scalar_tensor_tensor(
                                       out=ogtok[ri][:, hh * 256:(hh + 1) * 256], in0=pb[hp][:, (hh % 2) * 256:(hh % 2 + 1) * 256],
                                       scalar=rs2[ri][:, hh:hh + 1], in1=grs_v[:, ts, hh * 256:(hh + 1) * 256], op0=ALU.mult, op1=ALU.mult))

                    op("act", ["epsb"], [("ss2", 0, 0)], lambda h: h.activation(out=ln2[0][:, 0:1], in_=EPSB[:, 0:1], func=AF.Ln, bias=1.0))
                    seqs = [("AB", 0, 0), ("AB", 0, 1), ("C", 0, 0), ("C", 0, 1)]
                    for ts in range(1, NTS):
                        seqs += [("AB", ts, 0), ("E", ts - 1), ("AB", ts, 1), ("C", ts, 0), ("T", ts - 1), ("C", ts, 1)]
                    seqs += [("E", NTS - 1), ("N",), ("T", NTS - 1)]
                    for st in seqs:
                        if st[0] == "AB":
                            AB(st[1], st[2])
                        elif st[0] == "C":
                            C(st[1], st[2])
                        elif st[0] == "E":
                            E(st[1])
                        elif st[0] == "T":
                            emit_T(st[1])
                        yield

                def gen_conv(l=l):
                    ui = 0
                    gti = [0]
                    for hf in range(2):
                        sC, wC = get_block("C%d" % hf)
                        sX, wX = get_block("X%d" % hf)
                        sB, wB = get_block("B%d" % hf)
                        for c4 in range(4):
                            cc = hf * 4 + c4
                            u = ui % 2
                            ui += 1
                            b = bank()
                            fm_group(b, wC, ("ws", sC), c4 * 128, hT, hk)
                            op("act", [BK(b)], [("csb", u), BK(b)], lambda h, b=b, u=u: h.activation(out=csb[u][:], in_=pb[b][:, :], func=AF.Copy))
                            if c4 == 3:
                                done_block(sC)
                            yield
                            b = bank()
                            fm_group(b, wX, ("ws", sX), c4 * 128, hT, hk)
                            if c4 == 3:
                                done_block(sX)
                            op("dve", [BK(b), ("csb", u)], [("ub", u), BK(b)],
                               lambda h, b=b, u=u: h.tensor_tensor(out=ub[u][:, 2:T + 2], in0=pb[b][:, :], in1=csb[u][:], op=ALU.mult))
                            op("act", [("halo", l)], [("ub", u)], lambda h, u=u, cc=cc: h.activation(out=ub[u][:, 0:2], in_=halo[l][:, cc, :], func=AF.Copy))
                            op("act", [("ub", u)], [("halo", l)], lambda h, u=u, cc=cc: h.activation(out=halo[l][:, cc, :], in_=ub[u][:, T:T + 2], func=AF.Copy))
                            cw = lambda j, cc=cc: pkc(32 + (l * 3 + j) * 8 + cc)
                            op("dve", [("ub", u), "pk"], [("cacc", u)],
                               lambda h, u=u, cc=cc, cw=cw: h.tensor_scalar(out=cacc[u][:], in0=ub[u][:, 2:T + 2], scalar1=cw(2), scalar2=pkc(80 + l * 8 + cc),
                                                                          op0=ALU.mult, op1=ALU.add))
                            op("dve", [("ub", u), ("cacc", u), "pk"], [("cacc", u)],
                               lambda h, u=u, cw=cw: h.scalar_tensor_tensor(out=cacc[u][:], in0=ub[u][:, 1:T + 1], scalar=cw(1), in1=cacc[u][:],
                                                                            op0=ALU.mult, op1=ALU.add))
                            op("dve", [("ub", u), ("cacc", u), "pk"], [("cacc", u)],
                               lambda h, u=u, cw=cw: h.scalar_tensor_tensor(out=cacc[u][:], in0=ub[u][:, 0:T], scalar=cw(0), in1=cacc[u][:],
                                                                            op0=ALU.mult, op1=ALU.add))
                            yield
                            b = bank()
                            fm_group(b, wB, ("ws", sB), c4 * 128, hT, hk)
                            op("dve", [BK(b), ("cacc", u)], [("gbcT", cc), BK(b)],
                               lambda h, b=b, u=u, cc=cc: h.tensor_tensor(out=gbcT[:, cc, :], in0=pb[b][:, :], in1=cacc[u][:], op=ALU.mult))
                            yield
                        done_block(sB)
                    for nm, dstT in (("ga", tga), ("gb", tgb)):
                        for hf in range(2):
                            s, w = get_block("%s%d" % (nm, hf))
                            for c4 in range(4):
                                cc = hf * 4 + c4
                                b = bank()
                                fm_group(b, w, ("ws", s), c4 * 128, hT, hk)
                                gi = gti[0] % 2
                                gti[0] += 1
                                op("act", [BK(b)], [("eb", gi), BK(b)],
                                   lambda h, b=b, gi=gi: h.activation(out=eb[gi][:], in_=pb[b][:, :], func=AF.Exp, scale=-1.0))
                                op("act", [("eb", gi)], [("eb", gi)],
                                   lambda h, gi=gi: h.activation(out=eb[gi][:], in_=eb[gi][:], func=AF.Ln, bias=1.0))
                                op("act", [("eb", gi)], [(nm, cc)],
                                   lambda h, gi=gi, cc=cc, dstT=dstT: h.activation(out=dstT[:, cc, :], in_=eb[gi][:], func=AF.Exp, scale=-1.0))
                                yield
                            done_block(s)

                set_ring([4, 5, 6, 7])
                interleave(gen_gla(), gen_conv(), [2, 1, 2, 1, 2, 2])
                set_ring(list(range(8)))
                for hf in range(2):
                    sA, wA = get_block("oa%d" % hf)
                    sBb, wBb = get_block("ob%d" % hf)
                    for c4 in range(4):
                        dc = hf * 4 + c4
                        mi = (dc % 2) * 2
                        b = bank()
                        fm_group(b, wA, ("ws", sA), c4 * 128, ogT, lambda kc: [("ogT", ts_, kc % 2) for ts_ in range(NTS)])
                        op("dve", [BK(b), ("ga", dc)], [("spb", mi), BK(b)],
                           lambda h, b=b, dc=dc, mi=mi: h.tensor_tensor(out=spb[mi][:], in0=pb[b][:, :], in1=tga[:, dc, :], op=ALU.mult))
                        b = bank()
                        fm_group(b, wBb, ("ws", sBb), c4 * 128, gbcT, lambda kc: [("gbcT", kc)])
                        op("dve", [BK(b), ("gb", dc)], [("spb", mi + 1), BK(b)],
                           lambda h, b=b, dc=dc, mi=mi: h.tensor_tensor(out=spb[mi + 1][:], in0=pb[b][:, :], in1=tgb[:, dc, :], op=ALU.mult))
                        op("dve", [("spb", mi), ("spb", mi + 1)], [("mixT", dc)],
                           lambda h, dc=dc, mi=mi: h.tensor_tensor(out=mixT[:, dc, :], in0=spb[mi][:], in1=spb[mi + 1][:], op=ALU.add))
                    done_block(sA)
                    done_block(sBb)
                for hf in range(2):
                    s, w = get_block("o%d" % hf)
                    for c4 in range(4):
                        dc = hf * 4 + c4
                        b = bank()
                        fm_group(b, w, ("ws", s), c4 * 128, mixT, lambda kc: [("mixT", kc)])
                        op("dve", [BK(b), ("xT", dc)], [("xT", dc), BK(b)],
                           lambda h, b=b, dc=dc: h.tensor_tensor(out=xT[:, dc, :], in0=pb[b][:, :], in1=xT[:, dc, :], op=ALU.add))
                    done_block(s)

                def h2_out(dc, l=l):
                    op("dve", [("xT", dc), "rstd", "pk"], [("hT", dc)],
                       lambda h, dc=dc: h.scalar_tensor_tensor(out=hT[:, dc, :], in0=xT[:, dc, :], scalar=pkc(16 + l * 8 + dc),
                                                               in1=rstd[:], op0=ALU.mult, op1=ALU.mult))
                rmsnorm(None, h2_out)
                sli = 0
                for p in range(11):
                    s, w = get_block("f%d" % p)
                    if p == 0:
                        pre = [bank() for _ in range(4)]
                        fm_groups_kco([(pre[0], w, ("ws", s), 0, 128), (pre[1], w, ("ws", s), 256, 128),
                                       (pre[2], w, ("ws", s), 128, 128), (pre[3], w, ("ws", s), 384, 128)], hT, hk)
                    for j in range(2):
                        fc = p * 2 + j
                        if p == 0:
                            bg = pre[2 * j]
                        else:
                            bg = bank()
                            fm_group(bg, w, ("ws", s), j * 128, hT, hk)
                        si = sli % 2
                        sli += 1
                        op("act", [BK(bg)], [("csb", si), BK(bg)],
                           lambda h, bg=bg, si=si: h.activation(out=csb[si][:], in_=pb[bg][:, :], func=AF.Silu))
                        if p == 0:
                            bu = pre[2 * j + 1]
                        else:
                            bu = bank()
                            fm_group(bu, w, ("ws", s), 256 + j * 128, hT, hk)
                        op("dve", [BK(bu), ("csb", si)], AR(fc, 1) + [BK(bu)],
                           lambda h, bu=bu, si=si, fc=fc: h.tensor_tensor(out=hid_v[:, fc, :], in0=pb[bu][:, :], in1=csb[si][:], op=ALU.mult))
                    done_block(s)
                op("act", ["epsb"], [("ss2", 0, 0)], lambda h: h.activation(out=ln2[0][:, 0:1], in_=EPSB[:, 0:1], func=AF.Ln, bias=1.0))
                for dc in range(KC):
                    s, w = get_block("d%d" % dc)
                    b = bank()
                    mmg(BK(b), [(mm(pb[b][:, :], w[:, fc, :], hid_v[:, fc, :], fc == 0, fc == NFC - 1), [("ws", s)] + AR(fc, 1))
                                for fc in range(NFC)])
                    done_block(s)
                    op("dve", [BK(b), ("xT", dc)], [("xT", dc), BK(b)],
                       lambda h, b=b, dc=dc: h.tensor_tensor(out=xT[:, dc, :], in0=pb[b][:, :], in1=xT[:, dc, :], op=ALU.add))

            def f_stt(dc):
                mi = dc % 4
                op("dve", [("xT", dc), "rstd", "pk"], [("spb", mi)],
                   lambda h, dc=dc, mi=mi: h.scalar_tensor_tensor(out=spb[mi][:], in0=xT[:, dc, :], scalar=pkc(100 + dc),
                                                                   in1=rstd[:], op0=ALU.mult, op1=ALU.mult))

            def f_out(dc):
                mi = dc % 4
                if dc == 0:
                    for d2 in range(3):
                        f_stt(d2)
                if dc + 3 < KC:
                    f_stt(dc + 3)
                b = bank()
                mmg(BK(b), [(lambda pe, b=b, ts=ts, mi=mi: pe.transpose(pb[b][:, ts * 128:(ts + 1) * 128],
                                                                       spb[mi][:, ts * 128:(ts + 1) * 128], ident32[:]),
                             ["ident32", ("spb", mi)]) for ts in range(NTS)])
                src = pb[b][:, :].rearrange("p (a b) -> p a b", a=4)
                yi = dc % 3
                en = evac_engine()
                if en == "act":
                    op("act", [BK(b)], [("yst", yi), BK(b)],
                       lambda h, src=src, yi=yi: h.activation(out=yst[yi][:], in_=src, func=AF.Copy))
                else:
                    op("dve", [BK(b)], [("yst", yi), BK(b)],
                       lambda h, src=src, yi=yi: h.tensor_copy(out=yst[yi][:], in_=src))
                dma("pool", [("yst", yi)], [], ysem[yi],
                    lambda q, t0=t0, dc=dc, yi=yi: q.dma_start(out=y_d[t0:t0 + T, dc * 128:(dc + 1) * 128].rearrange("(a p) d -> p a d", p=128), in_=yst[yi][:]))
            def f_out2(dc, t=t):
                f_out(dc)
                if t + 1 < NT and dc % 2 == 1:
                    pc = dc // 2
                    x_transposes(pc)
                    if pc < 2:
                        x_load(t + 1, pc + 2, "act")
            rmsnorm(None, f_out2)
            if t + 2 < NT:
                x_load(t + 2, 0, "pool")
                x_load(t + 2, 1, "pool")

        e = tk.E["act"]
        for ys in ysem:
            e.h.wait_ge(ys[0], ys[1])
    return nc


def _host_inputs(inp):
    f = lambda a: np.ascontiguousarray(np.asarray(a, dtype=np.float32))
    pk = np.zeros((128, 108), np.float32)
    fm = lambda v: np.asarray(v, np.float32).reshape(-1, 128).T
    for l in range(DEPTH):
        pk[:, l * 8:(l + 1) * 8] = fm(inp["norm1_g"][l])
        pk[:, 16 + l * 8:16 + (l + 1) * 8] = fm(inp["norm2_g"][l])
        for j in range(3):
            pk[:, 32 + (l * 3 + j) * 8: 32 + (l * 3 + j + 1) * 8] = fm(inp["conv_w"][l][j])
        pk[:, 80 + l * 8:80 + (l + 1) * 8] = fm(inp["conv_b"][l])
        pk[:, 96 + l * 2:96 + (l + 1) * 2] = fm(inp["gla_norm_g"][l])
    pk[:, 100:108] = fm(inp["final_g"])
    wfg = np.concatenate([np.asarray(inp["w_fg2"], np.float32), np.asarray(inp["b_fg"], np.float32)[:, None, :]], axis=1)
    cst = np.zeros((128, 258), np.float32)
    cst[:, 0:128] = np.eye(128, dtype=np.float32)
    sp = np.arange(128)[:, None]
    s_ = np.arange(128)[None, :]
    cst[:, 128:256] = np.where((sp > s_) & (sp // 64 == s_ // 64), -1.0 / 16.0, 0.0)
    cst[:, 256:258] = np.where(sp // 64 == np.arange(2)[None, :], -1.0 / 16.0, 0.0)
    shared = {k: f(inp[k]) for k in ("w_in", "w_oa", "w_ob", "w_o", "w_ffn_gate", "w_ffn_up", "w_ffn_down")}
    shared["wfg"] = f(wfg)
    shared["pk"] = pk
    shared["cst"] = cst
    return shared


_NC_CACHE = {}


def kernel(**inp):
    x = np.asarray(inp["x"], np.float32)
    B, S, _ = x.shape
    shared = _host_inputs(inp)
    if S not in _NC_CACHE:
        _NC_CACHE[S] = build_nc(S)
    nc = _NC_CACHE[S]
    in_maps = []
    for b in range(B):
        m = dict(shared)
        m["x"] = np.ascontiguousarray(x[b])
        in_maps.append(m)
    res = run_bass_kernel_spmd(nc, in_maps, core_ids=list(range(B)))
    return np.stack([np.asarray(r["y"], np.float32) for r in res.results], axis=0)
```

```python
import numpy as np
import concourse.bass as bass
import concourse.mybir as mybir
from concourse.bass_utils import run_bass_kernel_spmd
from contextlib import ExitStack

F32 = mybir.dt.float32
BF16 = mybir.dt.bfloat16
AF = mybir.ActivationFunctionType
ALU = mybir.AluOpType

D = 1024
KC = 8
T = 512
NTS = 4
DEPTH = 2
INW = 8208
FF = 2816
NFC = 22
EPS = 1e-6
NW = 4
NCORES = 8


class _Eng:
    def __init__(self, name, h, sem):
        self.name, self.h, self.sem = name, h, sem
        self.cnt = 0
        self.waited = {}


class Trk:
    def __init__(self, nc, es):
        self.nc = nc
        self.es = es
        self.reg = {}
        self.E = {}
        for n, h in (("pe", nc.tensor), ("act", nc.scalar), ("dve", nc.vector),
                     ("pool", nc.gpsimd), ("sp", nc.sync)):
            self.E[n] = _Eng(n, h, es.enter_context(nc.semaphore("s_" + n)))
        self.nsem = 0

    def newsem(self, name):
        self.nsem += 1
        return self.es.enter_context(self.nc.semaphore(name))

    def deps(self, e, R, W):
        need = {}

        def add(ev, raw):
            if ev is None:
                return
            sem, val, src = ev
            if src == e.name:
                if e.name in ("pe", "sp"):
                    return
            k = id(sem)
            if k not in need or need[k][1] < val:
                need[k] = (sem, val)

        for k in R:
            st = self.reg.get(k)
            if st is not None:
                add(st[0], True)
        for k in W:
            st = self.reg.get(k)
            if st is not None:
                add(st[0], False)
                for ev in st[1].values():
                    add(ev, False)
        for k, (sem, val) in need.items():
            if e.waited.get(k, 0) < val:
                e.h.wait_ge(sem, val)
                e.waited[k] = val

    def record(self, R, W, ev):
        for k in R:
            st = self.reg.get(k)
            if st is None:
                st = [None, {}]
                self.reg[k] = st
            key = id(ev[0])
            old = st[1].get(key)
            if old is None or old[1] < ev[1]:
                st[1][key] = ev
        for k in W:
            self.reg[k] = [ev, {}]

    def op(self, en, R, W, f):
        e = self.E[en]
        self.deps(e, R, W)
        ins = f(e.h)
        ins.then_inc(e.sem, 1)
        e.cnt += 1
        self.record(R, W, (e.sem, e.cnt, en))

    def mm_group(self, bank_key, mms):
        e = self.E["pe"]
        n = len(mms)
        bks = bank_key if isinstance(bank_key, list) else [bank_key]
        for i, (f, R) in enumerate(mms):
            self.deps(e, R, bks if i == 0 else [])
            ins = f(e.h)
            if i == n - 1:
                ins.then_inc(e.sem, 1)
                e.cnt += 1
                ev = (e.sem, e.cnt, "pe")
            else:
                ev = (e.sem, e.cnt + 1, "pe")
            self.record(R, [], ev)
        self.record([], bks, (e.sem, e.cnt, "pe"))

    def dma(self, qn, R, W, dsem, f):
        e = self.E[qn]
        self.deps(e, R, W)
        ins = f(e.h)
        ins.then_inc(dsem[0], 16)
        dsem[1] += 16
        self.record(R, W, (dsem[0], dsem[1], "dma"))


def _block_list():
    bl = []
    inw = lambda c0, n: [("w_in", c0, n, 0)]
    bl.append(("fz", 8, 16, inw(3072, 16)))
    bl.append(("q", 8, 512, inw(0, 512)))
    bl.append(("v0", 8, 512, inw(1024, 512)))
    bl.append(("v1", 8, 512, inw(1536, 512)))
    bl.append(("r0", 8, 512, inw(2048, 512)))
    bl.append(("r1", 8, 512, inw(2560, 512)))
    bl.append(("k", 8, 512, inw(512, 512)))
    for hf in range(2):
        bl.append(("C%d" % hf, 8, 512, inw(4112 + hf * 512, 512)))
        bl.append(("X%d" % hf, 8, 512, inw(5136 + hf * 512, 512)))
        bl.append(("B%d" % hf, 8, 512, inw(3088 + hf * 512, 512)))
    for hf in range(2):
        bl.append(("ga%d" % hf, 8, 512, inw(6160 + hf * 512, 512)))
    for hf in range(2):
        bl.append(("gb%d" % hf, 8, 512, inw(7184 + hf * 512, 512)))
    for hf in range(2):
        bl.append(("oa%d" % hf, 8, 512, [("w_oa", hf * 512, 512, 0)]))
        bl.append(("ob%d" % hf, 8, 512, [("w_ob", hf * 512, 512, 0)]))
    for hf in range(2):
        bl.append(("o%d" % hf, 8, 512, [("w_o", hf * 512, 512, 0)]))
    for p in range(11):
        bl.append(("f%d" % p, 8, 512, [("w_ffn_gate", p * 256, 256, 0), ("w_ffn_up", p * 256, 256, 256)]))
    for dc in range(8):
        bl.append(("d%d" % dc, NFC, 128, [("w_ffn_down", dc * 128, 128, 0)]))
    return bl


def build_nc(S, depth=DEPTH):
    NT = S // T
    nc = bass.Bass("TRN2", target_bir_lowering=False)
    dt_in = lambda n, s: nc.dram_tensor(n, s, F32, kind="ExternalInput").ap()
    x_d = dt_in("x", [S, D])
    wd = {
        "w_in": dt_in("w_in", [DEPTH, D, INW]),
        "w_oa": dt_in("w_oa", [DEPTH, D, D]),
        "w_ob": dt_in("w_ob", [DEPTH, D, D]),
        "w_o": dt_in("w_o", [DEPTH, D, D]),
        "w_ffn_gate": dt_in("w_ffn_gate", [DEPTH, D, FF]),
        "w_ffn_up": dt_in("w_ffn_up", [DEPTH, D, FF]),
        "w_ffn_down": dt_in("w_ffn_down", [DEPTH, FF, D]),
    }
    wfg_d = dt_in("wfg", [DEPTH, 17, 512])
    pk_d = dt_in("pk", [128, 108])
    cst_d = dt_in("cst", [128, 258])
    y_d = nc.dram_tensor("y", [S, D], F32, kind="ExternalOutput").ap()

    blocks = _block_list()
    NB = len(blocks)
    scr = {}
    for l in range(depth):
        for bi, (bn, kcs, ncol, srcs) in enumerate(blocks):
            scr[(l, bi)] = nc.dram_tensor("sc_%d_%s" % (l, bn), [128, kcs * ncol], BF16).ap()

    with ExitStack() as es:
        tk = Trk(nc, es)
        op, mmg, dma = tk.op, tk.mm_group, tk.dma
        sb = lambda n, s, d: es.enter_context(nc.sbuf_tensor("sb_" + n, s, d))

        xT = sb("xT", [128, KC, T], F32)
        hT = sb("hT", [128, KC, T], BF16)
        wsl = [sb("wsl%d" % i, [128, 4096], BF16) for i in range(NW)]
        sqb = [sb("sqb%d" % i, [128, T], BF16) for i in range(2)]
        rstd = sb("rstd", [128, T], F32)
        qT = sb("qT", [128, 4, T], BF16)
        fza = sb("fza", [32, T], F32)
        eb = [sb("eb%d" % i, [128, T], F32) for i in range(2)]
        spb = [sb("spb%d" % i, [128, T], F32) for i in range(4)]
        gam = sb("gam", [128, NTS, 8], F32)
        kd = sb("kd", [128, NTS, 512], BF16)
        S32 = [[sb("S32_%d_%d" % (l, h), [128, 256], F32) for h in range(4)] for l in range(depth)]
        S16 = [sb("S16_%d" % i, [128, 256], BF16) for i in range(8)]
        ogtok = [sb("ogtok%d" % i, [128, 1024], BF16) for i in range(2)]
        ss = [sb("ss%d" % i, [128, 4], F32) for i in range(2)]
        rs2 = [sb("rs2%d" % i, [128, 4], F32) for i in range(2)]
        sqjunk = [sb("sqjunk%d" % i, [128, 256], BF16) for i in range(4)]
        ln2 = [sb("ln2%d" % i, [128, 4], F32) for i in range(2)]
        ogT = sb("ogT", [128, KC, T], BF16)
        csb = [sb("csb%d" % i, [128, T], F32) for i in range(2)]
        ub = [sb("ub%d" % i, [128, T + 2], F32) for i in range(2)]
        cacc = [sb("cacc%d" % i, [128, T], F32) for i in range(2)]
        halo = [sb("halo%d" % l, [128, KC, 2], F32) for l in range(depth)]
        gbcT = sb("gbcT", [128, KC, T], BF16)
        tga = sb("tga", [128, KC, T], BF16)
        tgb = sb("tgb", [128, KC, T], BF16)
        mixT = sb("mixT", [128, KC, T], BF16)
        ident32 = sb("ident32", [128, 128], F32)
        ident16 = sb("ident16", [128, 128], BF16)
        ones16 = sb("ones16", [128, 128], BF16)
        msk = sb("msk", [128, 130], F32)
        pk = sb("pk", [128, 108], F32)
        wfg = sb("wfg", [32, DEPTH, 512], F32)
        xst = [sb("xst%d" % i, [128, 4, 256], F32) for i in range(2)]
        yst = [sb("yst%d" % i, [128, 4, 128], F32) for i in range(3)]
        arena = sb("arena", [128, 6144], F32)

        def AR(i, n=1):
            return [("ar", j) for j in range(i, i + n)]

        a16 = arena[:, :].bitcast(BF16)
        v_v = a16[:, 0:4096].rearrange("p (a b) -> p a b", a=4)
        grs_v = a16[:, 4096:8192].rearrange("p (a b) -> p a b", a=4)
        dec_v = arena[:, 4096:6144].rearrange("p (a b) -> p a b", a=4)
        hid_v = a16[:, 0:NFC * 512].rearrange("p (a b) -> p a b", a=NFC)

        pb = [es.enter_context(nc.psum_tensor("pb%d" % i, [128, 512], F32)) for i in range(8)]
        bank_i = [0]
        ring = [list(range(8))]

        def set_ring(lst):
            ring[0] = lst
            bank_i[0] = 0

        def bank():
            i = bank_i[0] % len(ring[0])
            bank_i[0] = i + 1
            return ring[0][i]

        def interleave(ga, gb, nbp):
            a_done = b_done = False
            it = 0
            while not (a_done and b_done):
                if not a_done:
                    try:
                        next(ga)
                    except StopIteration:
                        a_done = True
                nb = nbp[it % len(nbp)] if isinstance(nbp, list) else nbp
                it += 1
                for _ in range(nb):
                    if not b_done:
                        try:
                            next(gb)
                        except StopIteration:
                            b_done = True

        BK = lambda i: ("bank", i)

        cs = [tk.newsem("cs"), 0]
        dma("sp", [], ["ident32"], cs, lambda q: q.dma_start(out=ident32[:], in_=cst_d[:, 0:128]))
        cs2 = [tk.newsem("cs2"), 0]
        dma("sp", [], ["msk"], cs2, lambda q: q.dma_start(out=msk[:], in_=cst_d[:, 128:258]))
        cs3 = [tk.newsem("cs3"), 0]
        dma("sp", [], ["pk"], cs3, lambda q: q.dma_start(out=pk[:], in_=pk_d))
        cs4 = [tk.newsem("cs4"), 0]
        dma("sp", [], ["wfg"], cs4, lambda q: q.dma_start(out=wfg[0:17, :, :], in_=wfg_d.rearrange("l r k -> r l k")))
        op("dve", ["ident32"], ["ident16"], lambda h: h.tensor_copy(out=ident16[:], in_=ident32[:]))
        op("dve", [], ["ones16"], lambda h: h.memset(ones16[:], 1.0 / D))
        op("dve", [], ["fza"], lambda h: h.memset(fza[:], 1.0))
        for l in range(depth):
            for hh in range(4):
                op("dve", [], [("S32", l, hh)], lambda h, l=l, hh=hh: h.memset(S32[l][hh][:], 0.0))
            op("dve", [], [("halo", l)], lambda h, l=l: h.memset(halo[l][:], 0.0))

        NPS = 16
        psem = [[tk.newsem("ps%d" % i), 0] for i in range(NPS)]
        pi = 0
        for l in range(depth):
            for bi, (bn, kcs, ncol, srcs) in enumerate(blocks):
                dst = scr[(l, bi)].rearrange("p (kc j) -> p kc j", kc=kcs)
                for si, (sn, c0, n, d0) in enumerate(srcs):
                    src = wd[sn][l][:, c0:c0 + n].rearrange("(kc p) j -> p kc j", p=128)
                    for ci, k0 in enumerate(range(0, kcs, 4)):
                        k1 = min(kcs, k0 + 4)
                        ds = psem[pi % (2 if pi < 8 else NPS)]
                        pi += 1
                        e = tk.E["pool"]
                        if ds[1] > 0 and e.waited.get(id(ds[0]), 0) < ds[1]:
                            e.h.wait_ge(ds[0], ds[1])
                            e.waited[id(ds[0])] = ds[1]
                        dma("pool", [], [("scr", l, bi, si, ci)], ds,
                            lambda q, dst=dst, src=src, d0=d0, n=n, k0=k0, k1=k1: q.dma_start(out=dst[:, k0:k1, d0:d0 + n], in_=src[:, k0:k1, :]))

        seq = [(t, l, bi) for t in range(NT) for l in range(depth) for bi in range(NB)]
        wst = {"next_load": 0, "free": list(range(NW)), "loaded": {}}
        wsem = [[tk.newsem("ws%d" % i), 0] for i in range(NW)]

        def issue_loads():
            while wst["free"] and wst["next_load"] < len(seq):
                j = wst["next_load"]
                wst["next_load"] += 1
                t, l, bi = seq[j]
                s = wst["free"].pop(0)
                bn, kcs, ncol, srcs = blocks[bi]
                R = [("scr", l, bi, si, ci) for si in range(len(srcs)) for ci in range((kcs + 3) // 4)]
                dma("sp", R, [("ws", s)], wsem[s],
                    lambda q, s=s, l=l, bi=bi, kcs=kcs, ncol=ncol: q.dma_start(out=wsl[s][:, 0:kcs * ncol], in_=scr[(l, bi)]))
                wst["loaded"][j] = s

        cur = [0]

        def get_block(name):
            j = cur[0]
            cur[0] += 1
            t, l, bi = seq[j]
            assert blocks[bi][0] == name, (blocks[bi][0], name)
            if j not in wst["loaded"]:
                issue_loads()
            s = wst["loaded"].pop(j)
            kcs, ncol = blocks[bi][1], blocks[bi][2]
            return s, wsl[s][:, 0:kcs * ncol].rearrange("p (kc j) -> p kc j", kc=kcs)

        def done_block(s):
            wst["free"].append(s)
            issue_loads()

        issue_loads()

        def pkc(c):
            return pk[:, c:c + 1]

        def mm(out, lhsT, rhs, first, last):
            return lambda pe: pe.matmul(out, lhsT, rhs, start=first, stop=last)

        def fm_group(b, w, ws_key, col0, rhsT, rkeys, m=128):
            mmg(BK(b), [(mm(pb[b][0:m, :], w[:, kc, col0:col0 + m], rhsT[:, kc, :], kc == 0, kc == KC - 1),
                         [ws_key] + rkeys(kc)) for kc in range(KC)])

        def fm_groups_kco(specs, rhsT, rkeys):
            e = tk.E["pe"]
            for kc in range(KC):
                for (b, w, ws_key, col0, m) in specs:
                    R = [ws_key] + rkeys(kc)
                    tk.deps(e, R, [BK(b)] if kc == 0 else [])
                    ins = e.h.matmul(pb[b][0:m, :], w[:, kc, col0:col0 + m], rhsT[:, kc, :], start=(kc == 0), stop=(kc == KC - 1))
                    if kc == KC - 1:
                        ins.then_inc(e.sem, 1)
                        e.cnt += 1
                        tk.record(R, [BK(b)], (e.sem, e.cnt, "pe"))
                    else:
                        tk.record(R, [], (e.sem, e.cnt + 1, "pe"))

        def tm_group(b, w, ws_key, ts, lhsT, lkeys):
            mmg(BK(b), [(mm(pb[b][:, :], lhsT[:, kc, ts * 128:(ts + 1) * 128], w[:, kc, :], kc == 0, kc == KC - 1),
                         [ws_key] + lkeys(kc)) for kc in range(KC)])

        hk = lambda kc: [("hT", kc)]
        sq_i = [0]

        def rmsnorm(gcol, out_fn):
            bn = bank()
            e = tk.E["pe"]
            for dc in range(KC):
                sqi = sq_i[0] % 2
                sq_i[0] += 1
                op("act", [("xT", dc)], [("sqb", sqi)],
                   lambda h, dc=dc, sqi=sqi: h.activation(out=sqb[sqi][:], in_=xT[:, dc, :], func=AF.Square))
                R = ["ones16", ("sqb", sqi)]
                tk.deps(e, R, [BK(bn)] if dc == 0 else [])
                ins = e.h.matmul(pb[bn][:, :], ones16[:, :], sqb[sqi][:], start=(dc == 0), stop=(dc == KC - 1))
                ins.then_inc(e.sem, 1)
                e.cnt += 1
                tk.record(R, [BK(bn)] if dc == KC - 1 else [], (e.sem, e.cnt, "pe"))
            op("act", [BK(bn)], ["rstd", BK(bn)],
               lambda h: h.activation(out=rstd[:], in_=pb[bn][:, :], func=AF.Ln, bias=EPSB[:, 0:1]))
            op("act", ["rstd"], ["rstd"], lambda h: h.activation(out=rstd[:], in_=rstd[:], func=AF.Exp, scale=-0.5))
            for dc in range(KC):
                out_fn(dc)

        EPSB = sb("epsb", [128, 1], F32)
        op("dve", [], ["epsb"], lambda h: h.memset(EPSB[:], EPS))

        alt = [0]

        def evac_engine():
            alt[0] += 1
            return "act" if alt[0] % 2 else "dve"

        ysem = [[tk.newsem("ys%d" % i), 0] for i in range(3)]
        xsem = {q: [[tk.newsem("xs%s%d" % (q, i)), 0] for i in range(2)] for q in ("act", "pool")}

        def x_load(t, pc, q):
            t0 = t * T
            dma(q, [], [("xst", pc % 2)], xsem[q][pc % 2],
                lambda qq, t0=t0, pc=pc: qq.dma_start(out=xst[pc % 2][:], in_=x_d[t0:t0 + T, pc * 256:(pc + 1) * 256].rearrange("(a p) d -> p a d", p=128)))

        def x_transposes(pc):
            for j in range(2):
                dc = 2 * pc + j
                b = bank()
                mmg(BK(b), [(lambda pe, b=b, ts=ts, j=j, pc=pc: pe.transpose(pb[b][:, ts * 128:(ts + 1) * 128],
                                                                           xst[pc % 2][:, ts, j * 128:(j + 1) * 128], ident32[:]),
                             ["ident32", ("xst", pc % 2)]) for ts in range(NTS)])
                en = evac_engine()
                if en == "act":
                    op("act", [BK(b)], [("xT", dc), BK(b)], lambda h, b=b, dc=dc: h.activation(out=xT[:, dc, :], in_=pb[b][:, :], func=AF.Copy))
                else:
                    op("dve", [BK(b)], [("xT", dc), BK(b)], lambda h, b=b, dc=dc: h.tensor_copy(out=xT[:, dc, :], in_=pb[b][:, :]))

        for t in range(NT):
            t0 = t * T
            if t == 0:
                x_load(0, 0, "act")
                x_load(0, 1, "act")
                x_transposes(0)
                x_load(0, 2, "act")
                x_transposes(1)
                x_load(0, 3, "act")
                x_transposes(2)
                x_transposes(3)
                if NT > 1:
                    x_load(1, 0, "pool")
                    x_load(1, 1, "pool")

            for l in range(depth):
                def h_out(dc, l=l):
                    op("dve", [("xT", dc), "rstd", "pk"], [("hT", dc)],
                       lambda h, dc=dc: h.scalar_tensor_tensor(out=hT[:, dc, :], in0=xT[:, dc, :], scalar=pkc(l * 8 + dc),
                                                               in1=rstd[:], op0=ALU.mult, op1=ALU.mult))
                rmsnorm(None, h_out)

                s, w = get_block("fz")
                sq_, wq_ = get_block("q")
                b = bank()
                qb = [bank() for _ in range(4)]
                fm_groups_kco([(b, w, ("ws", s), 0, 16)] + [(qb[hh], wq_, ("ws", sq_), hh * 128, 128) for hh in range(4)], hT, hk)
                done_block(s)
                done_block(sq_)
                op("dve", [BK(b)], ["fza", BK(b)], lambda h, b=b: h.tensor_copy(out=fza[0:16, :], in_=pb[b][0:16, :]))
                for hh in range(4):
                    op("act", [BK(qb[hh])], [("qT", hh), BK(qb[hh])],
                       lambda h, hh=hh, qb=qb: h.activation(out=qT[:, hh, :], in_=pb[qb[hh]][:, :], func=AF.Copy, scale=128.0 ** -0.5))
                for ts in range(NTS):
                    b = bank()
                    mmg(BK(b), [(mm(pb[b][:, :], fza[0:17, ts * 128:(ts + 1) * 128], wfg[0:17, l, :], True, True), ["fza", "wfg"])])
                    ei = ts % 2
                    op("act", [BK(b)], [("eb", ei), BK(b)], lambda h, b=b, ei=ei: h.activation(out=eb[ei][:], in_=pb[b][:, :], func=AF.Exp, scale=-1.0))
                    op("act", [("eb", ei)], [("spb", ts)], lambda h, ts=ts, ei=ei: h.activation(out=spb[ts][:], in_=eb[ei][:], func=AF.Ln, bias=1.0))

                def gen_gate(l=l):
                    for ts in range(NTS):
                        b = bank()
                        mmg(BK(b), [(mm(pb[b][:, :], msk[:, 0:128], spb[ts][:], True, True), ["msk", ("spb", ts)])])
                        op("act", [BK(b)], AR(16 + 2 * ts, 2) + [BK(b)],
                           lambda h, b=b, ts=ts: h.activation(out=dec_v[:, ts, :], in_=pb[b][:, :], func=AF.Exp))
                        b = bank()
                        mmg(BK(b), [(mm(pb[b][:, hh * 2:hh * 2 + 2], spb[ts][:, hh * 128:(hh + 1) * 128], msk[:, 128:130], True, True),
                                     ["msk", ("spb", ts)]) for hh in range(4)])
                        op("act", [BK(b)], [("gam", ts), BK(b)],
                           lambda h, b=b, ts=ts: h.activation(out=gam[:, ts, :], in_=pb[b][:, 0:8], func=AF.Exp))
                        yield

                def gen_inproj(l=l):
                    for hf in range(2):
                        s, w = get_block("v%d" % hf)
                        for ts in range(NTS):
                            b = bank()
                            tm_group(b, w, ("ws", s), ts, hT, hk)
                            en = evac_engine()
                            if en == "act":
                                op("act", [BK(b)], AR(2 * ts + hf, 1) + [BK(b)],
                                   lambda h, b=b, ts=ts, hf=hf: h.activation(out=v_v[:, ts, hf * 512:(hf + 1) * 512], in_=pb[b][:, :], func=AF.Copy))
                            else:
                                op("dve", [BK(b)], AR(2 * ts + hf, 1) + [BK(b)],
                                   lambda h, b=b, ts=ts, hf=hf: h.tensor_copy(out=v_v[:, ts, hf * 512:(hf + 1) * 512], in_=pb[b][:, :]))
                            yield
                        done_block(s)
                    for hf in range(2):
                        s, w = get_block("r%d" % hf)
                        for ts in range(NTS):
                            b = bank()
                            tm_group(b, w, ("ws", s), ts, hT, hk)
                            op("act", [BK(b)], AR(8 + 2 * ts + hf, 1) + [BK(b)],
                               lambda h, b=b, ts=ts, hf=hf: h.activation(out=grs_v[:, ts, hf * 512:(hf + 1) * 512], in_=pb[b][:, :], func=AF.Silu))
                            yield
                        done_block(s)

                interleave(gen_gate(), gen_inproj(), 2)
                s, w = get_block("k")
                for ts in range(NTS):
                    b = bank()
                    tm_group(b, w, ("ws", s), ts, hT, hk)
                    op("dve", [BK(b)] + AR(16 + 2 * ts, 2), [("kd", ts), BK(b)],
                       lambda h, b=b, ts=ts: h.tensor_tensor(out=kd[:, ts, :], in0=pb[b][:, :], in1=dec_v[:, ts, :], op=ALU.mult))
                done_block(s)

                def gen_gla(l=l):
                    s16c = [0]

                    def emit_T(ts):
                        ri = ts % 2
                        tb = 4
                        tb16 = pb[tb][:, :].bitcast(BF16)
                        mmg(BK(tb), [(lambda pe, vc=vc, ri=ri, tb16=tb16: pe.transpose(tb16[:, vc * 128:(vc + 1) * 128],
                                                                                     ogtok[ri][:, vc * 128:(vc + 1) * 128], ident16[:]),
                                      ["ident16", ("ogtok", ri, vc // 2)]) for vc in range(8)])
                        for par in range(2):
                            src = tb16.rearrange("p (a b c) -> p a b c", a=4, b=2)[:, :, par, :]
                            dst = ogT[:, :, ts * 128:(ts + 1) * 128].rearrange("p (a b) c -> p a b c", b=2)[:, :, par, :]
                            op("dve", [BK(tb), "pk"], [("ogT", ts, par), BK(tb)],
                               lambda h, src=src, dst=dst, par=par: h.tensor_scalar_mul(out=dst, in0=src, scalar1=pkc(96 + l * 2 + par)))

                    slots_of = {}

                    def AB(ts, c):
                        pr = slice(c * 64, (c + 1) * 64)
                        for hp in range(2):
                            ubk = 2 + hp
                            mmg(BK(ubk), [(mm(pb[ubk][:, h2 * 256:(h2 + 1) * 256], kd[pr, ts, (hp * 2 + h2) * 128:(hp * 2 + h2 + 1) * 128],
                                              v_v[pr, ts, (hp * 2 + h2) * 256:(hp * 2 + h2 + 1) * 256], True, True),
                                           [("kd", ts)] + AR(2 * ts + hp, 1)) for h2 in range(2)])
                        slots = []
                        for hp in range(2):
                            ubk = 2 + hp
                            for h2 in range(2):
                                hh = hp * 2 + h2
                                gcol = hh * 2 + c
                                op("dve", [BK(ubk), ("S32", l, hh), ("gam", ts)], [("S32", l, hh), BK(ubk)],
                                   lambda h, ubk=ubk, h2=h2, hh=hh, gcol=gcol, ts=ts: h.scalar_tensor_tensor(
                                       out=S32[l][hh][:], in0=S32[l][hh][:], scalar=gam[:, ts, gcol:gcol + 1],
                                       in1=pb[ubk][:, h2 * 256:(h2 + 1) * 256], op0=ALU.mult, op1=ALU.add))
                                si = s16c[0] % 8
                                s16c[0] += 1
                                op("act", [("S32", l, hh)], [("S16", si)],
                                   lambda h, si=si, hh=hh: h.activation(out=S16[si][:], in_=S32[l][hh][:], func=AF.Copy))
                                slots.append((hh, si))
                        slots_of[(ts, c)] = slots

                    def C(ts, c):
                        pr = slice(c * 64, (c + 1) * 64)
                        slots = slots_of.pop((ts, c))
                        for hp in range(2):
                            mmg(BK(hp), [(mm(pb[hp][pr, (hh % 2) * 256:(hh % 2 + 1) * 256],
                                             qT[:, hh, ts * 128 + c * 64: ts * 128 + (c + 1) * 64], S16[si][:], True, True),
                                          [("qT", hh), ("S16", si)]) for hh, si in slots[2 * hp:2 * hp + 2]])

                    def E(ts):
                        ri = ts % 2
                        for hp in range(2):
                            for hh in (2 * hp, 2 * hp + 1):
                                op("act", [BK(hp)], [("sqjunk", hh), ("ss", ri, hh), BK(hp)],
                                   lambda h, hh=hh, ri=ri, hp=hp: h.activation(out=sqjunk[hh][:], in_=pb[hp][:, (hh % 2) * 256:(hh % 2 + 1) * 256],
                                                                        func=AF.Square, accum_out=ss[ri][:, hh:hh + 1]))
                            op("act", [("ss", ri, 2 * hp), ("ss", ri, 2 * hp + 1)], [("ss2", ri, hp)],
                               lambda h, ri=ri, hp=hp: h.activation(out=ln2[ri][:, 2 * hp:2 * hp + 2], in_=ss[ri][:, 2 * hp:2 * hp + 2], func=AF.Ln,
                                                                    scale=1.0 / 256, bias=EPSB[:, 0:1]))
                            op("act", [("ss2", ri, hp)], [("rs2", ri, hp)],
                               lambda h, ri=ri, hp=hp: h.activation(out=rs2[ri][:, 2 * hp:2 * hp + 2], in_=ln2[ri][:, 2 * hp:2 * hp + 2], func=AF.Exp, scale=-0.5))
                            for hh in (2 * hp, 2 * hp + 1):
                                op("dve", [BK(hp), ("rs2", ri, hp)] + AR(8 + 2 * ts + hp, 1), [("ogtok", ri, hh), BK(hp)],
                                   lambda h, hh=hh, ri=ri, ts=ts, hp=hp: h.scalar_tensor_tensor(
                                       out=ogtok[ri][:, hh * 256:(hh + 1) * 256], in0=pb[hp][:, (hh % 2) * 256:(hh % 2 + 1) * 256],
                                       scalar=rs2[ri][:, hh:hh + 1], in1=grs_v[:, ts, hh * 256:(hh + 1) * 256], op0=ALU.mult, op1=ALU.mult))

                    op("act", ["epsb"], [("ss2", 0, 0)], lambda h: h.activation(out=ln2[0][:, 0:1], in_=EPSB[:, 0:1], func=AF.Ln, bias=1.0))
                    seqs = [("AB", 0, 0), ("AB", 0, 1), ("C", 0, 0), ("C", 0, 1)]
                    for ts in range(1, NTS):
                        seqs += [("AB", ts, 0), ("E", ts - 1), ("AB", ts, 1), ("C", ts, 0), ("T", ts - 1), ("C", ts, 1)]
                    seqs += [("E", NTS - 1), ("N",), ("T", NTS - 1)]
                    for st in seqs:
                        if st[0] == "AB":
                            AB(st[1], st[2])
                        elif st[0] == "C":
                            C(st[1], st[2])
                        elif st[0] == "E":
                            E(st[1])
                        elif st[0] == "T":
                            emit_T(st[1])
                        yield

                def gen_conv(l=l):
                    ui = 0
                    gti = [0]
                    for hf in range(2):
                        sC, wC = get_block("C%d" % hf)
                        sX, wX = get_block("X%d" % hf)
                        sB, wB = get_block("B%d" % hf)
                        for c4 in range(4):
                            cc = hf * 4 + c4
                            u = ui % 2
                            ui += 1
                            b = bank()
                            fm_group(b, wC, ("ws", sC), c4 * 128, hT, hk)
                            op("act", [BK(b)], [("csb", u), BK(b)], lambda h, b=b, u=u: h.activation(out=csb[u][:], in_=pb[b][:, :], func=AF.Copy))
                            if c4 == 3:
                                done_block(sC)
                            yield
                            b = bank()
                            fm_group(b, wX, ("ws", sX), c4 * 128, hT, hk)
                            if c4 == 3:
                                done_block(sX)
                            op("dve", [BK(b), ("csb", u)], [("ub", u), BK(b)],
                               lambda h, b=b, u=u: h.tensor_tensor(out=ub[u][:, 2:T + 2], in0=pb[b][:, :], in1=csb[u][:], op=ALU.mult))
                            op("act", [("halo", l)], [("ub", u)], lambda h, u=u, cc=cc: h.activation(out=ub[u][:, 0:2], in_=halo[l][:, cc, :], func=AF.Copy))
                            op("act", [("ub", u)], [("halo", l)], lambda h, u=u, cc=cc: h.activation(out=halo[l][:, cc, :], in_=ub[u][:, T:T + 2], func=AF.Copy))
                            cw = lambda j, cc=cc: pkc(32 + (l * 3 + j) * 8 + cc)
                            op("dve", [("ub", u), "pk"], [("cacc", u)],
                               lambda h, u=u, cc=cc, cw=cw: h.tensor_scalar(out=cacc[u][:], in0=ub[u][:, 2:T + 2], scalar1=cw(2), scalar2=pkc(80 + l * 8 + cc),
                                                                          op0=ALU.mult, op1=ALU.add))
                            op("dve", [("ub", u), ("cacc", u), "pk"], [("cacc", u)],
                               lambda h, u=u, cw=cw: h.scalar_tensor_tensor(out=cacc[u][:], in0=ub[u][:, 1:T + 1], scalar=cw(1), in1=cacc[u][:],
                                                                            op0=ALU.mult, op1=ALU.add))
                            op("dve", [("ub", u), ("cacc", u), "pk"], [("cacc", u)],
                               lambda h, u=u, cw=cw: h.scalar_tensor_tensor(out=cacc[u][:], in0=ub[u][:, 0:T], scalar=cw(0), in1=cacc[u][:],
                                                                            op0=ALU.mult, op1=ALU.add))
                            yield
                            b = bank()
                            fm_group(b, wB, ("ws", sB), c4 * 128, hT, hk)
                            op("dve", [BK(b), ("cacc", u)], [("gbcT", cc), BK(b)],
                               lambda h, b=b, u=u, cc=cc: h.tensor_tensor(out=gbcT[:, cc, :], in0=pb[b][:, :], in1=cacc[u][:], op=ALU.mult))
                            yield
                        done_block(sB)
                    for nm, dstT in (("ga", tga), ("gb", tgb)):
                        for hf in range(2):
                            s, w = get_block("%s%d" % (nm, hf))
                            for c4 in range(4):
                                cc = hf * 4 + c4
                                b = bank()
                                fm_group(b, w, ("ws", s), c4 * 128, hT, hk)
                                gi = gti[0] % 2
                                gti[0] += 1
                                op("act", [BK(b)], [("eb", gi), BK(b)],
                                   lambda h, b=b, gi=gi: h.activation(out=eb[gi][:], in_=pb[b][:, :], func=AF.Exp, scale=-1.0))
                                op("act", [("eb", gi)], [("eb", gi)],
                                   lambda h, gi=gi: h.activation(out=eb[gi][:], in_=eb[gi][:], func=AF.Ln, bias=1.0))
                                op("act", [("eb", gi)], [(nm, cc)],
                                   lambda h, gi=gi, cc=cc, dstT=dstT: h.activation(out=dstT[:, cc, :], in_=eb[gi][:], func=AF.Exp, scale=-1.0))
                                yield
                            done_block(s)

                set_ring([5, 6, 7])
                interleave(gen_gla(), gen_conv(), [2, 1, 2, 1, 2, 2])
                set_ring(list(range(8)))
                for hf in range(2):
                    sA, wA = get_block("oa%d" % hf)
                    sBb, wBb = get_block("ob%d" % hf)
                    for c4 in range(4):
                        dc = hf * 4 + c4
                        mi = (dc % 2) * 2
                        b = bank()
                        fm_group(b, wA, ("ws", sA), c4 * 128, ogT, lambda kc: [("ogT", ts_, kc % 2) for ts_ in range(NTS)])
                        op("dve", [BK(b), ("ga", dc)], [("spb", mi), BK(b)],
                           lambda h, b=b, dc=dc, mi=mi: h.tensor_tensor(out=spb[mi][:], in0=pb[b][:, :], in1=tga[:, dc, :], op=ALU.mult))
                        b = bank()
                        fm_group(b, wBb, ("ws", sBb), c4 * 128, gbcT, lambda kc: [("gbcT", kc)])
                        op("dve", [BK(b), ("gb", dc)], [("spb", mi + 1), BK(b)],
                           lambda h, b=b, dc=dc, mi=mi: h.tensor_tensor(out=spb[mi + 1][:], in0=pb[b][:, :], in1=tgb[:, dc, :], op=ALU.mult))
                        op("dve", [("spb", mi), ("spb", mi + 1)], [("mixT", dc)],
                           lambda h, dc=dc, mi=mi: h.tensor_tensor(out=mixT[:, dc, :], in0=spb[mi][:], in1=spb[mi + 1][:], op=ALU.add))
                    done_block(sA)
                    done_block(sBb)
                for hf in range(2):
                    s, w = get_block("o%d" % hf)
                    for c4 in range(4):
                        dc = hf * 4 + c4
                        b = bank()
                        fm_group(b, w, ("ws", s), c4 * 128, mixT, lambda kc: [("mixT", kc)])
                        op("dve", [BK(b), ("xT", dc)], [("xT", dc), BK(b)],
                           lambda h, b=b, dc=dc: h.tensor_tensor(out=xT[:, dc, :], in0=pb[b][:, :], in1=xT[:, dc, :], op=ALU.add))
                    done_block(s)

                def h2_out(dc, l=l):
                    op("dve", [("xT", dc), "rstd", "pk"], [("hT", dc)],
                       lambda h, dc=dc: h.scalar_tensor_tensor(out=hT[:, dc, :], in0=xT[:, dc, :], scalar=pkc(16 + l * 8 + dc),
                                                               in1=rstd[:], op0=ALU.mult, op1=ALU.mult))
                rmsnorm(None, h2_out)
                op("act", ["epsb"], [("ss2", 0, 0)], lambda h: h.activation(out=ln2[0][:, 0:1], in_=EPSB[:, 0:1], func=AF.Silu))
                sli = 0
                for p in range(11):
                    s, w = get_block("f%d" % p)
                    if p == 0:
                        pre = [bank() for _ in range(4)]
                        fm_groups_kco([(pre[0], w, ("ws", s), 0, 128), (pre[1], w, ("ws", s), 256, 128),
                                       (pre[2], w, ("ws", s), 128, 128), (pre[3], w, ("ws", s), 384, 128)], hT, hk)
                    for j in range(2):
                        fc = p * 2 + j
                        if p == 0:
                            bg = pre[2 * j]
                        else:
                            bg = bank()
                            fm_group(bg, w, ("ws", s), j * 128, hT, hk)
                        si = sli % 2
                        sli += 1
                        op("act", [BK(bg)], [("csb", si), BK(bg)],
                           lambda h, bg=bg, si=si: h.activation(out=csb[si][:], in_=pb[bg][:, :], func=AF.Silu))
                        if p == 0:
                            bu = pre[2 * j + 1]
                        else:
                            bu = bank()
                            fm_group(bu, w, ("ws", s), 256 + j * 128, hT, hk)
                        op("dve", [BK(bu), ("csb", si)], AR(fc, 1) + [BK(bu)],
                           lambda h, bu=bu, si=si, fc=fc: h.tensor_tensor(out=hid_v[:, fc, :], in0=pb[bu][:, :], in1=csb[si][:], op=ALU.mult))
                    done_block(s)
                op("act", ["epsb"], [("ss2", 0, 0)], lambda h: h.activation(out=ln2[0][:, 0:1], in_=EPSB[:, 0:1], func=AF.Ln, bias=1.0))
                for dc in range(KC):
                    s, w = get_block("d%d" % dc)
                    b = bank()
                    mmg(BK(b), [(mm(pb[b][:, :], w[:, fc, :], hid_v[:, fc, :], fc == 0, fc == NFC - 1), [("ws", s)] + AR(fc, 1))
                                for fc in range(NFC)])
                    done_block(s)
                    op("dve", [BK(b), ("xT", dc)], [("xT", dc), BK(b)],
                       lambda h, b=b, dc=dc: h.tensor_tensor(out=xT[:, dc, :], in0=pb[b][:, :], in1=xT[:, dc, :], op=ALU.add))

            def f_stt(dc):
                mi = dc % 4
                op("dve", [("xT", dc), "rstd", "pk"], [("spb", mi)],
                   lambda h, dc=dc, mi=mi: h.scalar_tensor_tensor(out=spb[mi][:], in0=xT[:, dc, :], scalar=pkc(100 + dc),
                                                                   in1=rstd[:], op0=ALU.mult, op1=ALU.mult))

            def f_out(dc):
                mi = dc % 4
                if dc == 0:
                    for d2 in range(3):
                        f_stt(d2)
                if dc + 3 < KC:
                    f_stt(dc + 3)
                b = bank()
                mmg(BK(b), [(lambda pe, b=b, ts=ts, mi=mi: pe.transpose(pb[b][:, ts * 128:(ts + 1) * 128],
                                                                       spb[mi][:, ts * 128:(ts + 1) * 128], ident32[:]),
                             ["ident32", ("spb", mi)]) for ts in range(NTS)])
                src = pb[b][:, :].rearrange("p (a b) -> p a b", a=4)
                yi = dc % 3
                en = evac_engine()
                if en == "act":
                    op("act", [BK(b)], [("yst", yi), BK(b)],
                       lambda h, src=src, yi=yi: h.activation(out=yst[yi][:], in_=src, func=AF.Copy))
                else:
                    op("dve", [BK(b)], [("yst", yi), BK(b)],
                       lambda h, src=src, yi=yi: h.tensor_copy(out=yst[yi][:], in_=src))
                dma("pool", [("yst", yi)], [], ysem[yi],
                    lambda q, t0=t0, dc=dc, yi=yi: q.dma_start(out=y_d[t0:t0 + T, dc * 128:(dc + 1) * 128].rearrange("(a p) d -> p a d", p=128), in_=yst[yi][:]))
            def f_out2(dc, t=t):
                f_out(dc)
                if t + 1 < NT and dc % 2 == 1:
                    pc = dc // 2
                    x_transposes(pc)
                    if pc < 2:
                        x_load(t + 1, pc + 2, "act")
            rmsnorm(None, f_out2)
            if t + 2 < NT:
                x_load(t + 2, 0, "pool")
                x_load(t + 2, 1, "pool")

        e = tk.E["act"]
        for ys in ysem:
            e.h.wait_ge(ys[0], ys[1])
    return nc


def _host_inputs(inp):
    f = lambda a: np.ascontiguousarray(np.asarray(a, dtype=np.float32))
    pk = np.zeros((128, 108), np.float32)
    fm = lambda v: np.asarray(v, np.float32).reshape(-1, 128).T
    for l in range(DEPTH):
        pk[:, l * 8:(l + 1) * 8] = fm(inp["norm1_g"][l])
        pk[:, 16 + l * 8:16 + (l + 1) * 8] = fm(inp["norm2_g"][l])
        for j in range(3):
            pk[:, 32 + (l * 3 + j) * 8: 32 + (l * 3 + j + 1) * 8] = fm(inp["conv_w"][l][j])
        pk[:, 80 + l * 8:80 + (l + 1) * 8] = fm(inp["conv_b"][l])
        pk[:, 96 + l * 2:96 + (l + 1) * 2] = fm(inp["gla_norm_g"][l])
    pk[:, 100:108] = fm(inp["final_g"])
    wfg = np.concatenate([np.asarray(inp["w_fg2"], np.float32), np.asarray(inp["b_fg"], np.float32)[:, None, :]], axis=1)
    cst = np.zeros((128, 258), np.float32)
    cst[:, 0:128] = np.eye(128, dtype=np.float32)
    sp = np.arange(128)[:, None]
    s_ = np.arange(128)[None, :]
    cst[:, 128:256] = np.where((sp > s_) & (sp // 64 == s_ // 64), -1.0 / 16.0, 0.0)
    cst[:, 256:258] = np.where(sp // 64 == np.arange(2)[None, :], -1.0 / 16.0, 0.0)
    shared = {k: f(inp[k]) for k in ("w_in", "w_oa", "w_ob", "w_o", "w_ffn_gate", "w_ffn_up", "w_ffn_down")}
    shared["wfg"] = f(wfg)
    shared["pk"] = pk
    shared["cst"] = cst
    return shared


_NC_CACHE = {}


def kernel(**inp):
    x = np.asarray(inp["x"], np.float32)
    B, S, _ = x.shape
    shared = _host_inputs(inp)
    if S not in _NC_CACHE:
        _NC_CACHE[S] = build_nc(S)
    nc = _NC_CACHE[S]
    in_maps = []
    for b in range(B):
        m = dict(shared)
        m["x"] = np.ascontiguousarray(x[b])
        in_maps.append(m)
    res = run_bass_kernel_spmd(nc, in_maps, core_ids=list(range(B)))
    return np.stack([np.asarray(r["y"], np.float32) for r in res.results], axis=0)
```
